# Optimizing a Trainium2 kernel written in Bass

```python
import math
import jax, jax.numpy as jnp
from jax import lax
import numpy as np

D_MODEL = 2048
BATCH = 4
SEQ = 8192
DEPTH = 4

GRID_W = 64
CTX_LEN = 256
NORM_EPS = 1e-6
N_MOD = 6
HEAD_DIM = 128
N_Q_HEADS = D_MODEL // 256
N_KV_HEADS = 2
GQA_GROUP = N_Q_HEADS // N_KV_HEADS
ATTN_WIDTH = N_Q_HEADS * HEAD_DIM
KV_WIDTH = N_KV_HEADS * HEAD_DIM
ATTN_SCALE = HEAD_DIM ** -0.5
Q_BLOCK = 128
ROPE_AXIS_DIM = HEAD_DIM // 2
ROPE_BASE = 10000.0
CONV_WIDTH = D_MODEL // 2
CONV_K = 3
SSM_GROUP = 16
SSM_WIDTH = 3 * D_MODEL // 8
SSM_GROUPS = SSM_WIDTH // SSM_GROUP
SSM_STATE = 64
SSM_RE_MAX = -1e-4
N_BRANCH = 3
D_FF = 4 * D_MODEL
PROJ_WIDTHS = (CONV_WIDTH, CONV_WIDTH, CONV_WIDTH, SSM_WIDTH, ATTN_WIDTH, KV_WIDTH, KV_WIDTH, N_BRANCH * D_MODEL)
IN_WIDTH = sum(PROJ_WIDTHS)

kernel_name = "hybrid_conv_s5_gqa_parallel_diffusion_block"


def rms_norm(x, g):
    xf = x.astype(jnp.float32)
    y = xf * lax.rsqrt(jnp.mean(xf * xf, axis=-1, keepdims=True) + NORM_EPS)
    return (y * g.astype(jnp.float32)).astype(x.dtype)


def modulate(h, shift, scale):
    return h * (1 + scale) + shift


def split_proj(p):
    offs = np.cumsum(PROJ_WIDTHS)[:-1].tolist()
    return jnp.split(p, offs, axis=-1)


def axial_rope_tables(n, dtype):
    rows = n // GRID_W
    row = jnp.repeat(jnp.arange(rows), GRID_W)
    col = jnp.tile(jnp.arange(GRID_W), rows)
    half = ROPE_AXIS_DIM // 2
    inv_freq = ROPE_BASE ** (-jnp.arange(half, dtype=jnp.float32) / half)
    ang_r = row.astype(jnp.float32)[:, None] * inv_freq
    ang_c = col.astype(jnp.float32)[:, None] * inv_freq
    return (jnp.cos(ang_r).astype(dtype), jnp.sin(ang_r).astype(dtype),
            jnp.cos(ang_c).astype(dtype), jnp.sin(ang_c).astype(dtype))


def rotate_half_rope(x, cos, sin):
    x1, x2 = jnp.split(x, 2, axis=-1)
    cos = cos[None, :, None, :]
    sin = sin[None, :, None, :]
    return jnp.concatenate([x1 * cos - x2 * sin, x2 * cos + x1 * sin], axis=-1)


def apply_axial_rope(x, tables):
    cos_r, sin_r, cos_c, sin_c = tables
    x_row, x_col = jnp.split(x, 2, axis=-1)
    return jnp.concatenate([rotate_half_rope(x_row, cos_r, sin_r), rotate_half_rope(x_col, cos_c, sin_c)], axis=-1)


def gqa_attend(q, k, v):
    b, lq = q.shape[:2]
    qg = q.reshape(b, lq, N_KV_HEADS, GQA_GROUP, HEAD_DIM)
    s = jnp.einsum('bqhgd,bkhd->bhgqk', qg, k).astype(jnp.float32) * ATTN_SCALE
    p = jax.nn.softmax(s, axis=-1).astype(v.dtype)
    o = jnp.einsum('bhgqk,bkhd->bqhgd', p, v)
    return o.reshape(b, lq, ATTN_WIDTH)


def blocked_attention(q, k, v):
    b, n = q.shape[:2]
    nblk = n // Q_BLOCK
    qb = jnp.moveaxis(q.reshape(b, nblk, Q_BLOCK, N_Q_HEADS, HEAD_DIM), 1, 0)
    o = lax.map(lambda qq: gqa_attend(qq, k, v), qb)
    return jnp.moveaxis(o, 0, 1).reshape(b, n, ATTN_WIDTH)


def dwconv3_centred(u, w):
    n = u.shape[1]
    up = jnp.pad(u, ((0, 0), (1, 1), (0, 0)))
    return up[:, :n] * w[0] + up[:, 1:n + 1] * w[1] + up[:, 2:] * w[2]


def short_conv_branch(gate_b, gate_c, v, conv_w, w_conv_out):
    return (gate_b * dwconv3_centred(gate_c * v, conv_w)) @ w_conv_out


def ssm_discretise(lam_re, lam_im, log_dt, b_re, b_im):
    lam_re = jnp.minimum(lam_re.astype(jnp.float32), SSM_RE_MAX)
    lam_im = lam_im.astype(jnp.float32)
    dt = jnp.exp(log_dt.astype(jnp.float32))[:, None]
    mag = jnp.exp(lam_re * dt)
    ab_re = mag * jnp.cos(lam_im * dt)
    ab_im = mag * jnp.sin(lam_im * dt)
    nr = ab_re - 1
    den = lam_re * lam_re + lam_im * lam_im
    f_re = (nr * lam_re + ab_im * lam_im) / den
    f_im = (ab_im * lam_re - nr * lam_im) / den
    b_re = b_re.astype(jnp.float32)
    b_im = b_im.astype(jnp.float32)
    bb_re = f_re[..., None] * b_re - f_im[..., None] * b_im
    bb_im = f_re[..., None] * b_im + f_im[..., None] * b_re
    return ab_re, ab_im, bb_re, bb_im


def complex_linear_combine(e1, e2):
    a1r, a1i, b1r, b1i = e1
    a2r, a2i, b2r, b2i = e2
    return (a2r * a1r - a2i * a1i, a2r * a1i + a2i * a1r,
            a2r * b1r - a2i * b1i + b2r, a2r * b1i + a2i * b1r + b2i)


def ssm_states(u, disc, h0):
    ab_re, ab_im, bb_re, bb_im = disc
    bu_re = jnp.einsum('blgp,gnp->blgn', u, bb_re)
    bu_im = jnp.einsum('blgp,gnp->blgn', u, bb_im)
    if h0 is not None:
        h_re, h_im = h0
        bu_re = bu_re.at[:, 0].add(ab_re * h_re - ab_im * h_im)
        bu_im = bu_im.at[:, 0].add(ab_re * h_im + ab_im * h_re)
    l = u.shape[1]
    a_re = jnp.broadcast_to(ab_re, (1, l) + ab_re.shape)
    a_im = jnp.broadcast_to(ab_im, (1, l) + ab_im.shape)
    _, _, h_re, h_im = lax.associative_scan(complex_linear_combine, (a_re, a_im, bu_re, bu_im), axis=1)
    return h_re, h_im


def ssm_readout(h_re, h_im, c_re, c_im):
    return jnp.einsum('blgn,gpn->blgp', h_re, c_re) - jnp.einsum('blgn,gpn->blgp', h_im, c_im)


def glu_out(y, w_glu):
    a, g = jnp.split(jax.nn.gelu(y) @ w_glu, 2, axis=-1)
    return a * jax.nn.sigmoid(g)


def ssm_branch(u_x, u_c, lam_re, lam_im, log_dt, b_re, b_im, c_re, c_im, d_skip, w_glu, need_ctx):
    dtype = u_x.dtype
    bx, n, _ = u_x.shape
    bc, m, _ = u_c.shape
    ux = u_x.astype(jnp.float32).reshape(bx, n, SSM_GROUPS, SSM_GROUP)
    uc = u_c.astype(jnp.float32).reshape(bc, m, SSM_GROUPS, SSM_GROUP)
    dsk = d_skip.astype(jnp.float32).reshape(SSM_GROUPS, SSM_GROUP)
    y_x = dsk * ux
    y_c = dsk * uc if need_ctx else None
    for d in range(2):
        disc = ssm_discretise(lam_re[d], lam_im[d], log_dt[d], b_re[d], b_im[d])
        cr = c_re[d].astype(jnp.float32)
        ci = c_im[d].astype(jnp.float32)
        sx = ux if d == 0 else jnp.flip(ux, axis=1)
        sc = uc if d == 0 else jnp.flip(uc, axis=1)
        hc_re, hc_im = ssm_states(sc, disc, None)
        hx_re, hx_im = ssm_states(sx, disc, (hc_re[:, -1], hc_im[:, -1]))
        yx = ssm_readout(hx_re, hx_im, cr, ci)
        y_x = y_x + (yx if d == 0 else jnp.flip(yx, axis=1))
        if need_ctx:
            yc = ssm_readout(hc_re, hc_im, cr, ci)
            y_c = y_c + (yc if d == 0 else jnp.flip(yc, axis=1))
    out_x = glu_out(y_x.reshape(bx, n, SSM_WIDTH).astype(dtype), w_glu)
    out_c = glu_out(y_c.reshape(bc, m, SSM_WIDTH).astype(dtype), w_glu) if need_ctx else None
    return out_x, out_c


def gated_merge(y_conv, y_ssm, y_attn, gate_logits, w_out):
    g = jax.nn.sigmoid(gate_logits.astype(jnp.float32)).astype(y_conv.dtype)
    g_conv, g_ssm, g_attn = jnp.split(g, N_BRANCH, axis=-1)
    return (g_conv * y_conv + g_ssm * y_ssm + g_attn * y_attn) @ w_out


def hybrid_mixer(hx, hc, w_in, conv_w, w_conv_out, lam_re, lam_im, log_dt, b_re, b_im, c_re, c_im,
                 d_skip, w_glu, q_gain, k_gain, w_attn_out, w_out, rope, need_ctx):
    bx, n, _ = hx.shape
    bc, m, _ = hc.shape
    xa_b, xa_c, xa_v, xs_u, xq, xk, xv, xg = split_proj(hx @ w_in)
    ca_b, ca_c, ca_v, cs_u, cq, ck, cv, cg = split_proj(hc @ w_in)
    ya_x = short_conv_branch(xa_b, xa_c, xa_v, conv_w, w_conv_out)
    ys_x, ys_c = ssm_branch(xs_u, cs_u, lam_re, lam_im, log_dt, b_re, b_im, c_re, c_im, d_skip, w_glu, need_ctx)
    q_x = apply_axial_rope(rms_norm(xq.reshape(bx, n, N_Q_HEADS, HEAD_DIM), q_gain), rope)
    k_x = apply_axial_rope(rms_norm(xk.reshape(bx, n, N_KV_HEADS, HEAD_DIM), k_gain), rope)
    v_x = xv.reshape(bx, n, N_KV_HEADS, HEAD_DIM)
    k_c = rms_norm(ck.reshape(bc, m, N_KV_HEADS, HEAD_DIM), k_gain)
    v_c = cv.reshape(bc, m, N_KV_HEADS, HEAD_DIM)
    k_all = jnp.concatenate([k_c, k_x], axis=1)
    v_all = jnp.concatenate([v_c, v_x], axis=1)
    yc_x = blocked_attention(q_x, k_all, v_all) @ w_attn_out
    out_x = gated_merge(ya_x, ys_x, yc_x, xg, w_out)
    if not need_ctx:
        return out_x, None
    ya_c = short_conv_branch(ca_b, ca_c, ca_v, conv_w, w_conv_out)
    q_c = rms_norm(cq.reshape(bc, m, N_Q_HEADS, HEAD_DIM), q_gain)
    yc_c = gqa_attend(q_c, k_c, v_c) @ w_attn_out
    out_c = gated_merge(ya_c, ys_c, yc_c, cg, w_out)
    return out_x, out_c


def sq_relu_mlp(h, w_up, w_down):
    return jnp.square(jax.nn.relu(h @ w_up)) @ w_down


def setup_inputs(seed: int = 0) -> dict:
    key = jax.random.key(seed)
    ks = jax.random.split(key, 32)
    f32 = jnp.float32

    def nrm(k, shape, s):
        return jax.random.normal(k, shape, f32) * s

    g_shape = (DEPTH, SSM_GROUPS, SSM_STATE)
    n_idx = jnp.arange(SSM_STATE, dtype=f32)
    return {
        "x": nrm(ks[0], (BATCH, SEQ, D_MODEL), 1.0),
        "c": nrm(ks[1], (BATCH, D_MODEL), 1.0),
        "ctx": nrm(ks[2], (BATCH, CTX_LEN, D_MODEL), 1.0),
        "c_ctx": nrm(ks[3], (D_MODEL,), 1.0),
        "w_mod": nrm(ks[4], (DEPTH, D_MODEL, N_MOD * D_MODEL), 0.5 * D_MODEL ** -0.5),
        "b_mod": nrm(ks[5], (DEPTH, N_MOD * D_MODEL), 0.02),
        "g_pre_mix": 1.0 + nrm(ks[6], (DEPTH, D_MODEL), 0.05),
        "g_post_mix": 1.0 + nrm(ks[7], (DEPTH, D_MODEL), 0.05),
        "g_pre_mlp": 1.0 + nrm(ks[8], (DEPTH, D_MODEL), 0.05),
        "g_post_mlp": 1.0 + nrm(ks[9], (DEPTH, D_MODEL), 0.05),
        "w_in": nrm(ks[10], (DEPTH, D_MODEL, IN_WIDTH), D_MODEL ** -0.5),
        "conv_w": nrm(ks[11], (DEPTH, CONV_K, CONV_WIDTH), CONV_K ** -0.5),
        "w_conv_out": nrm(ks[12], (DEPTH, CONV_WIDTH, D_MODEL), CONV_WIDTH ** -0.5),
        "ssm_lam_re": -0.5 + nrm(ks[13], (DEPTH, 2, SSM_GROUPS, SSM_STATE), 0.01),
        "ssm_lam_im": math.pi * n_idx + nrm(ks[14], (DEPTH, 2, SSM_GROUPS, SSM_STATE), 0.01),
        "ssm_log_dt": jax.random.uniform(ks[15], (DEPTH, 2, SSM_GROUPS), f32, math.log(1e-3), math.log(1e-1)),
        "ssm_b_re": nrm(ks[16], (DEPTH, 2, SSM_GROUPS, SSM_STATE, SSM_GROUP), (2 * SSM_GROUP) ** -0.5),
        "ssm_b_im": nrm(ks[17], (DEPTH, 2, SSM_GROUPS, SSM_STATE, SSM_GROUP), (2 * SSM_GROUP) ** -0.5),
        "ssm_c_re": nrm(ks[18], (DEPTH, 2, SSM_GROUPS, SSM_GROUP, SSM_STATE), SSM_STATE ** -0.5),
        "ssm_c_im": nrm(ks[19], (DEPTH, 2, SSM_GROUPS, SSM_GROUP, SSM_STATE), SSM_STATE ** -0.5),
        "ssm_d": nrm(ks[20], (DEPTH, SSM_WIDTH), 1.0),
        "w_glu": nrm(ks[21], (DEPTH, SSM_WIDTH, 2 * D_MODEL), SSM_WIDTH ** -0.5),
        "q_gain": 1.0 + nrm(ks[22], (DEPTH, HEAD_DIM), 0.05),
        "k_gain": 1.0 + nrm(ks[23], (DEPTH, HEAD_DIM), 0.05),
        "w_attn_out": nrm(ks[24], (DEPTH, ATTN_WIDTH, D_MODEL), ATTN_WIDTH ** -0.5),
        "w_out": nrm(ks[25], (DEPTH, D_MODEL, D_MODEL), D_MODEL ** -0.5),
        "w_up": nrm(ks[26], (DEPTH, D_MODEL, D_FF), D_MODEL ** -0.5),
        "w_down": nrm(ks[27], (DEPTH, D_FF, D_MODEL), D_FF ** -0.5),
    }


def reference(x, c, ctx, c_ctx, w_mod, b_mod, g_pre_mix, g_post_mix, g_pre_mlp, g_post_mlp, w_in, conv_w,
              w_conv_out, ssm_lam_re, ssm_lam_im, ssm_log_dt, ssm_b_re, ssm_b_im, ssm_c_re, ssm_c_im, ssm_d,
              w_glu, q_gain, k_gain, w_attn_out, w_out, w_up, w_down):
    b, n, _ = x.shape
    rope = axial_rope_tables(n, x.dtype)
    for l in range(DEPTH):
        need_ctx = l < DEPTH - 1
        mod_x = (jax.nn.silu(c) @ w_mod[l] + b_mod[l]).reshape(b, 1, N_MOD, D_MODEL)
        mod_c = (jax.nn.silu(c_ctx) @ w_mod[l] + b_mod[l]).reshape(1, 1, N_MOD, D_MODEL)
        hx = modulate(rms_norm(x, g_pre_mix[l]), mod_x[:, :, 0], mod_x[:, :, 1])
        hc = modulate(rms_norm(ctx, g_pre_mix[l]), mod_c[:, :, 0], mod_c[:, :, 1])
        mx, mc = hybrid_mixer(hx, hc, w_in[l], conv_w[l], w_conv_out[l], ssm_lam_re[l], ssm_lam_im[l],
                              ssm_log_dt[l], ssm_b_re[l], ssm_b_im[l], ssm_c_re[l], ssm_c_im[l], ssm_d[l],
                              w_glu[l], q_gain[l], k_gain[l], w_attn_out[l], w_out[l], rope, need_ctx)
        x = x + mod_x[:, :, 2] * rms_norm(mx, g_post_mix[l])
        hx = modulate(rms_norm(x, g_pre_mlp[l]), mod_x[:, :, 3], mod_x[:, :, 4])
        x = x + mod_x[:, :, 5] * rms_norm(sq_relu_mlp(hx, w_up[l], w_down[l]), g_post_mlp[l])
        if need_ctx:
            ctx = ctx + mod_c[:, :, 2] * rms_norm(mc, g_post_mix[l])
            hc = modulate(rms_norm(ctx, g_pre_mlp[l]), mod_c[:, :, 3], mod_c[:, :, 4])
            ctx = ctx + mod_c[:, :, 5] * rms_norm(sq_relu_mlp(hc, w_up[l], w_down[l]), g_post_mlp[l])
    return x
```

```python
import math
import numpy as np
import ml_dtypes
import concourse.bass as bass
import concourse.mybir as mybir
from concourse.bass_utils import run_bass_kernel_spmd

F32 = mybir.dt.float32
BF16 = mybir.dt.bfloat16
I32 = mybir.dt.int32
AF = mybir.ActivationFunctionType
ALU = mybir.AluOpType

D = 2048
KC = 16
CTX = 256
GRID_W = 64
EPS = 1e-6
HD = 128
NQH = 8
NKV = 2
CONV_W = 1024
SSM_W = 768
NG = 48
NST = 64
DFF = 8192
IN_W = 11520
OFF_GB, OFF_GC, OFF_V, OFF_U, OFF_Q, OFF_K, OFF_VA, OFF_G = 0, 1024, 2048, 3072, 3840, 4864, 5120, 5376
ATTN_SCALE = HD ** -0.5
NPOW = 18
POW_LIST = [1, 2, 3, 4, 5, 6, 7] + [8 * (1 << i) for i in range(11)]
POW_IDX = {p: i for i, p in enumerate(POW_LIST)}
TWO_PI = 2.0 * math.pi


class Buf:
    __slots__ = ("w", "r")

    def __init__(self):
        self.w = {}
        self.r = {}


class Tile:
    def __init__(self, t, nb=1):
        self.t = t
        self.bufs = [Buf() for _ in range(nb)]
        self.b = self.bufs[0]

    def __getitem__(self, idx):
        return self.t[idx]


class Rot:
    def __init__(self, tiles):
        self.tiles = tiles
        self.i = 0

    def next(self):
        t = self.tiles[self.i % len(self.tiles)]
        self.i += 1
        return t


class Prog:
    def __init__(self, nc):
        self.nc = nc
        self.E = {"pe": nc.tensor, "act": nc.scalar, "dve": nc.vector, "pool": nc.gpsimd, "sp": nc.sync}
        self.sems = []
        self.cs = {}
        self.cnt = {}
        for e in ("pe", "act", "dve"):
            self.cs[e] = self._sem("c_" + e)
            self.cnt[e] = 0
        self.waited = {e: {} for e in self.E}
        self.rings = {"sp": [self._sem(f"rsp{i}") for i in range(30)],
                      "pool": [self._sem(f"rpl{i}") for i in range(30)]}
        self.ring_n = {"sp": 0, "pool": 0}
        self.ring_cnt = {q: [0] * len(r) for q, r in self.rings.items()}
        self.n_ops = 0

    def _sem(self, name):
        self.sems.append(self.nc.alloc_semaphore(name))
        return len(self.sems) - 1

    def _need(self, reads, writes):
        need = {}
        for b in reads:
            for s, v in b.w.items():
                if need.get(s, 0) < v:
                    need[s] = v
        for b in writes:
            for d in (b.w, b.r):
                for s, v in d.items():
                    if need.get(s, 0) < v:
                        need[s] = v
        return need

    def _wait(self, eng, need):
        e = self.E[eng]
        w = self.waited[eng]
        for s, v in need.items():
            if w.get(s, 0) >= v:
                continue
            e.wait_ge(self.sems[s], v)
            w[s] = v

    def _record(self, s, v, reads, writes):
        for b in writes:
            b.w = {s: v}
            b.r = {}
        for b in reads:
            if b in writes:
                continue
            if b.r.get(s, 0) < v:
                b.r[s] = v

    def op(self, eng, fn, reads=(), writes=()):
        need = self._need(reads, writes)
        if eng == "pe":
            need.pop(self.cs["pe"], None)
        self._wait(eng, need)
        ins = fn(self.E[eng])
        self.cnt[eng] += 1
        ins.then_inc(self.sems[self.cs[eng]], 1)
        self._record(self.cs[eng], self.cnt[eng], reads, writes)
        self.n_ops += 1

    def dma(self, q, out, in_, reads=(), writes=()):
        need = self._need(reads, writes)
        ring = self.rings[q]
        i = self.ring_n[q] % len(ring)
        self.ring_n[q] += 1
        s = ring[i]
        prev = self.ring_cnt[q][i]
        if prev and need.get(s, 0) < prev:
            need[s] = prev
        self._wait(q, need)
        self.E[q].dma_start(out=out, in_=in_).then_inc(self.sems[s], 16)
        self.ring_cnt[q][i] = prev + 16
        self._record(s, prev + 16, reads, writes)
        self.n_ops += 1

    def barrier(self):
        need = {}
        for e in self.cs:
            if self.cnt[e]:
                need[self.cs[e]] = self.cnt[e]
        for q, ring in self.rings.items():
            for i, s in enumerate(ring):
                if self.ring_cnt[q][i]:
                    need[s] = self.ring_cnt[q][i]
        for eng in self.E:
            self._wait(eng, dict(need))


class Ctx:
    pass


_UID = [0]


def sb_tile(g, stack, name, shape, dtype, nb=1):
    _UID[0] += 1
    t = stack.enter_context(g.nc.sbuf_tensor(f"{name}_u{_UID[0]}", shape, dtype))
    return Tile(t, nb)


def sb_rot(g, stack, name, shape, dtype, n, nb=1):
    return Rot([sb_tile(g, stack, f"{name}_{i}", shape, dtype, nb) for i in range(n)])


def mm_group(pg, out_ap, pairs, reads, writes, start=True, stop=True):
    def fn(e):
        ins = None
        n = len(pairs)
        for i, (l, r) in enumerate(pairs):
            ins = e.matmul(out_ap, l, r, start=(start and i == 0), stop=(stop and i == n - 1))
        return ins
    pg.op("pe", fn, reads=reads, writes=writes)


def token_blocks(seq):
    blks = [(0, CTX, 1)]
    for i in range(seq // 512):
        blks.append((CTX + 512 * i, 512, 0))
    return blks


def build_program(seq, depth):
    from contextlib import ExitStack
    T = CTX + seq
    L = depth
    nc = bass.Bass("TRN2", target_bir_lowering=False)
    pg = Prog(nc)
    g = Ctx()
    g.nc = nc
    g.pg = pg
    g.T = T
    g.seq = seq
    g.L = L
    g.blocks = token_blocks(seq)

    def din(name, shape, dt=F32):
        return nc.dram_tensor(name, list(shape), dt, kind="ExternalInput").ap()

    def dscr(name, shape, dt):
        return nc.dram_tensor(name, list(shape), dt, kind="Internal").ap()

    I = {}
    I["x0T"] = din("x0T", [D, T])
    I["cT"] = din("cT", [128, KC, 2])
    I["w_mod"] = din("w_mod", [L, D, 6 * D])
    I["b_modT"] = din("b_modT", [L, 128, 96, 2])
    for nm in ("g_pre_mix", "g_post_mix", "g_pre_mlp", "g_post_mlp"):
        I[nm] = din(nm, [L, 128, KC])
    I["w_in"] = din("w_in", [L, D, IN_W])
    I["conv_w"] = din("conv_w", [L, 128, 8, 3])
    I["w_conv_out"] = din("w_conv_out", [L, CONV_W, D])
    for nm in ("lam_re_a", "lam_im_a", "log_dt_a"):
        I[nm] = din(nm, [L, 128, 96])
    for nm in ("lam_re_b", "lam_im_b", "log_dt_b"):
        I[nm] = din(nm, [L, 16, 96 * NST])
    I["b_re_b"] = din("b_re_b", [L, 16, 96 * NST])
    I["b_im_b"] = din("b_im_b", [L, 16, 96 * NST])
    I["c_ri"] = din("c_ri", [L, 128, 96 * 16])
    I["ssm_dT"] = din("ssm_dT", [L, 16, NG])
    I["w_glu"] = din("w_glu", [L, SSM_W, 2 * D])
    I["q_gain"] = din("q_gain", [L, 128, 1])
    I["k_gain"] = din("k_gain", [L, 128, 1])
    I["w_attn_out"] = din("w_attn_out", [L, 1024, D])
    I["w_out"] = din("w_out", [L, D, D])
    I["w_up"] = din("w_up", [L, D, DFF])
    I["w_down"] = din("w_down", [L, DFF, D])
    I["rope_cos"] = din("rope_cos", [128, seq])
    I["rope_sin"] = din("rope_sin", [128, seq])
    I["c_ident"] = din("c_ident", [128, 128])
    I["c_jmat"] = din("c_jmat", [128, 128])
    I["c_perm"] = din("c_perm", [128, 128])
    g.I = I
    g.out = nc.dram_tensor("outT", [D, seq], F32, kind="ExternalOutput").ap()

    S = {}
    S["XA"] = dscr("XA", [D, T], F32)
    S["XB"] = dscr("XB", [D, T], F32)
    S["XM"] = dscr("XM", [D, T], F32)
    S["PRJ"] = dscr("PRJ", [IN_W, T], BF16)
    S["VTOK"] = dscr("VTOK", [T, 256], BF16)
    S["YS"] = dscr("YS", [SSM_W, T], BF16)
    S["AT"] = dscr("AT", [1024, T], BF16)
    S["MRG"] = dscr("MRG", [D, T], BF16)
    S["WIN"] = dscr("WIN", [L, D, IN_W], BF16)
    S["WCO"] = dscr("WCO", [L, CONV_W, D], BF16)
    S["WGL"] = dscr("WGL", [L, SSM_W, 2 * D], BF16)
    S["WAO"] = dscr("WAO", [L, 1024, D], BF16)
    S["WO"] = dscr("WO", [L, D, D], BF16)
    S["WUP"] = dscr("WUP", [L, D, DFF], BF16)
    S["WDN"] = dscr("WDN", [L, DFF, D], BF16)
    g.S = S
    g.db = {k: Buf() for k in ("XA", "XB", "XM", "PRJ", "VTOK", "YS", "AT", "MRG", "OUT")}
    g.wbuf = {(k, l): Buf() for k in ("WIN", "WCO", "WGL", "WAO", "WO", "WUP", "WDN") for l in range(L)}
    g.inbuf = Buf()

    with ExitStack() as top:
        g.psum = [Tile(top.enter_context(nc.psum_tensor(f"ps{i}", [128, 512], F32))) for i in range(8)]
        g.ident = sb_tile(g, top, "ident", [128, 128], BF16)
        g.ident_f = sb_tile(g, top, "ident_f", [128, 128], F32)
        g.jmat_f = sb_tile(g, top, "jmat_f", [128, 128], F32)
        g.perm = sb_tile(g, top, "perm", [128, 128], BF16)
        g.ones = sb_tile(g, top, "ones", [128, 128], BF16)
        g.sc = sb_tile(g, top, "sc", [128, KC, 2], BF16)
        g.mod = [sb_tile(g, top, f"mod{l}", [128, 96, 2], F32) for l in range(L)]
        g.coef = [sb_tile(g, top, f"coef{l}", [128, 6, KC, 2], F32) for l in range(L)]
        g.epsb = sb_tile(g, top, "epsb", [128, 1], F32)

        emit_setup(g)
        cast_gens = [cast_ops(g, l) for l in range(L)]
        for thunk in cast_gens[0]:
            thunk()
        for l in range(L):
            xin_name = "x0T" if l == 0 else ("XA" if l % 2 == 1 else "XB")
            xout_name = "XA" if l % 2 == 0 else "XB"
            nxt = cast_gens[l + 1] if l + 1 < L else iter(())
            emit_pass1(g, l, xin_name, nxt)
            for thunk in nxt:
                thunk()
            emit_ssm(g, l)
            emit_attn(g, l)
            emit_merge(g, l)
            emit_wout(g, l, xin_name)
            emit_mlp(g, l, xout_name, last=(l == L - 1))
        pg.barrier()
    return nc, pg


def xin_ap(g, name):
    return g.I["x0T"] if name == "x0T" else g.S[name]


def xin_buf(g, name):
    return g.inbuf if name == "x0T" else g.db[name]


def emit_setup(g):
    from contextlib import ExitStack
    pg, nc, I = g.pg, g.nc, g.I
    with ExitStack() as st:
        tmpf = sb_tile(g, st, "su_tmpf", [128, 128], F32)
        cf = sb_tile(g, st, "su_cf", [128, KC, 2], F32)
        wm = sb_rot(g, st, "su_wm", [128, KC, 512], BF16, 2)
        bm = sb_tile(g, st, "su_bm", [128, 96, 2], F32)
        gv = sb_tile(g, st, "su_gv", [128, 4, KC], F32)
        pg.dma("sp", g.ident_f[:, :], I["c_ident"][:, :], reads=[g.inbuf], writes=[g.ident_f.b])
        pg.dma("sp", g.jmat_f[:, :], I["c_jmat"][:, :], reads=[g.inbuf], writes=[g.jmat_f.b])
        pg.dma("sp", tmpf[:, :], I["c_perm"][:, :], reads=[g.inbuf], writes=[tmpf.b])
        pg.op("dve", lambda e: e.tensor_copy(out=g.ident[:, :], in_=g.ident_f[:, :]), reads=[g.ident_f.b], writes=[g.ident.b])
        pg.op("dve", lambda e: e.tensor_copy(out=g.perm[:, :], in_=tmpf[:, :]), reads=[tmpf.b], writes=[g.perm.b])
        pg.op("dve", lambda e: e.memset(g.ones[:, :], 1.0), writes=[g.ones.b])
        pg.op("dve", lambda e: e.memset(g.epsb[:, :], EPS), writes=[g.epsb.b])
        pg.dma("sp", cf[:, :, :], I["cT"][:, :, :], reads=[g.inbuf], writes=[cf.b])
        pg.op("act", lambda e: e.activation(out=g.sc[:, :, :], in_=cf[:, :, :], func=AF.Silu), reads=[cf.b], writes=[g.sc.b])
        for l in range(g.L):
            ps = g.psum[l % 8]
            for nb in range(24):
                w = wm.next()
                src = I["w_mod"][l].rearrange("(k p) n -> p k n", p=128)[:, :, nb * 512:(nb + 1) * 512]
                pg.dma("pool", w[:, :, :], src, reads=[g.inbuf], writes=[w.b])
                for j in range(4):
                    m = nb * 4 + j
                    pairs = [(w[:, k, j * 128:(j + 1) * 128], g.sc[:, k, :]) for k in range(KC)]
                    mm_group(pg, ps[:, 2 * m:2 * m + 2], pairs, reads=[w.b, g.sc.b], writes=[ps.b])
            pg.dma("sp", bm[:, :, :], I["b_modT"][l], reads=[g.inbuf], writes=[bm.b])
            md = g.mod[l]
            pg.op("dve", lambda e: e.tensor_tensor(out=md[:, :, :], in0=ps[:, 0:192].rearrange("p (m v) -> p m v", v=2),
                                                   in1=bm[:, :, :], op=ALU.add),
                  reads=[ps.b, bm.b], writes=[md.b])
            for i, nm in enumerate(("g_pre_mix", "g_post_mix", "g_pre_mlp", "g_post_mlp")):
                pg.dma("sp", gv[:, i, :], I[nm][l], reads=[g.inbuf], writes=[gv.b])
            cf6 = g.coef[l]
            for v in range(2):
                for (ci, mscale, mshift, mgate, gpre, gpost) in ((0, 1, 0, 2, 0, 1), (3, 4, 3, 5, 2, 3)):
                    pg.op("dve", lambda e, ci=ci, mscale=mscale, gpre=gpre, v=v: e.scalar_tensor_tensor(
                        out=cf6[:, ci, :, v], in0=md[:, mscale * 16:(mscale + 1) * 16, v], scalar=1.0,
                        in1=gv[:, gpre, :], op0=ALU.add, op1=ALU.mult), reads=[md.b, gv.b], writes=[cf6.b])
                    pg.op("dve", lambda e, ci=ci, mshift=mshift, v=v: e.tensor_copy(
                        out=cf6[:, ci + 1, :, v], in_=md[:, mshift * 16:(mshift + 1) * 16, v]), reads=[md.b], writes=[cf6.b])
                    pg.op("dve", lambda e, ci=ci, mgate=mgate, gpost=gpost, v=v: e.tensor_tensor(
                        out=cf6[:, ci + 2, :, v], in0=md[:, mgate * 16:(mgate + 1) * 16, v],
                        in1=gv[:, gpost, :], op=ALU.mult), reads=[md.b, gv.b], writes=[cf6.b])
        pg.barrier()


def cast_ops(g, l):
    pg = g.pg
    specs = (("WIN", "w_in", D), ("WCO", "w_conv_out", CONV_W), ("WGL", "w_glu", SSM_W), ("WAO", "w_attn_out", 1024),
             ("WO", "w_out", D), ("WUP", "w_up", D), ("WDN", "w_down", DFF))
    for sname, iname, rows in specs:
        for r0 in range(0, rows, 128):
            def thunk(sname=sname, iname=iname, r0=r0):
                pg.dma("pool", g.S[sname][l, r0:r0 + 128, :], g.I[iname][l, r0:r0 + 128, :],
                       reads=[g.inbuf], writes=[g.wbuf[(sname, l)]])
            yield thunk


def rstd_from_ss(g, ss_ps, n, rstd, tmp, denom):
    pg = g.pg
    pg.op("act", lambda e: e.activation(out=tmp[:, :n], in_=ss_ps[:, :n], func=AF.Sqrt, bias=g.epsb[:, 0:1], scale=1.0 / denom),
          reads=[ss_ps.b, g.epsb.b], writes=[tmp.b])
    pg.op("dve", lambda e: e.reciprocal(out=rstd[:, :n], in_=tmp[:, :n]), reads=[tmp.b], writes=[rstd.b])


def emit_pass1(g, l, xin_name, cast_next):
    from contextlib import ExitStack
    pg, nc, I, S = g.pg, g.nc, g.I, g.S
    T = g.T
    XIN = xin_ap(g, xin_name).rearrange("(k p) t -> p k t", p=128)
    xb = xin_buf(g, xin_name)
    WIN = S["WIN"][l].rearrange("(k p) n -> p k n", p=128)
    wbuf = g.wbuf[("WIN", l)]
    PRJ = S["PRJ"]
    cf6 = g.coef[l]
    with ExitStack() as st:
        xin = sb_rot(g, st, "p1_xin", [128, KC, 512], F32, 2)
        sq = sb_tile(g, st, "p1_sq", [128, KC, 512], BF16)
        hx = sb_rot(g, st, "p1_hx", [128, KC, 512], BF16, 2)
        wt = sb_rot(g, st, "p1_wt", [128, KC, 512], BF16, 2)
        stg = sb_rot(g, st, "p1_stg", [128, 4, 512], BF16, 3)
        vst = sb_rot(g, st, "p1_vst", [128, 4, 256], BF16, 2)
        cst = sb_rot(g, st, "p1_cos", [128, 512], F32, 2)
        sst = sb_rot(g, st, "p1_sin", [128, 512], F32, 2)
        rstd = sb_rot(g, st, "p1_rstd", [128, 512], F32, 2)
        tmpa = sb_rot(g, st, "p1_tmpa", [128, 512], F32, 3)
        tmpb = sb_rot(g, st, "p1_tmpb", [128, 512], F32, 3)
        sqh = sb_rot(g, st, "p1_sqh", [128, 512], BF16, 2)
        qn = sb_rot(g, st, "p1_qn", [128, 512], BF16, 2)
        gains = sb_tile(g, st, "p1_gain", [128, 2], F32)
        pg.dma("sp", gains[:, 0:1], I["q_gain"][l], reads=[g.inbuf], writes=[gains.b])
        pg.dma("sp", gains[:, 1:2], I["k_gain"][l], reads=[g.inbuf], writes=[gains.b])
        psr = Rot(g.psum[0:5])
        psx = Rot(g.psum[5:8])
        ncast = 0
        for (n0, n, isctx) in g.blocks:
            for _ in range(9):
                th = next(cast_next, None)
                if th is not None:
                    th()
            x = xin.next()
            pg.dma("sp", x[:, :, :n], XIN[:, :, n0:n0 + n], reads=[xb], writes=[x.b])
            if not isctx:
                ct = cst.next()
                sn = sst.next()
                pg.dma("sp", ct[:, :n], I["rope_cos"][:, n0 - CTX:n0 - CTX + n], reads=[g.inbuf], writes=[ct.b])
                pg.dma("sp", sn[:, :n], I["rope_sin"][:, n0 - CTX:n0 - CTX + n], reads=[g.inbuf], writes=[sn.b])
            pg.op("act", lambda e: e.activation(out=sq[:, :, :n], in_=x[:, :, :n], func=AF.Square), reads=[x.b], writes=[sq.b])
            ss = psx.next()
            mm_group(pg, ss[:, :n], [(g.ones[:, :], sq[:, k, :n]) for k in range(KC)], reads=[sq.b, g.ones.b], writes=[ss.b])
            rs = rstd.next()
            t0 = tmpa.next()
            rstd_from_ss(g, ss, n, rs, t0, float(D))
            h = hx.next()
            for k in range(KC):
                t1 = tmpa.next()
                pg.op("dve", lambda e, k=k, t1=t1: e.scalar_tensor_tensor(
                    out=t1[:, :n], in0=x[:, k, :n], scalar=cf6[:, 0, k, isctx:isctx + 1], in1=rs[:, :n],
                    op0=ALU.mult, op1=ALU.mult), reads=[x.b, rs.b, cf6.b], writes=[t1.b])
                pg.op("act", lambda e, k=k, t1=t1: e.activation(
                    out=h[:, k, :n], in_=t1[:, :n], func=AF.Identity, bias=cf6[:, 1, k, isctx:isctx + 1], scale=1.0),
                    reads=[t1.b, cf6.b], writes=[h.b])
            for wbk in range(23):
                c0 = wbk * 512
                cw = min(512, IN_W - c0)
                w = wt.next()
                pg.dma("sp", w[:, :, :cw], WIN[:, :, c0:c0 + cw], reads=[wbuf], writes=[w.b])
                sg = None
                for j in range(cw // 128):
                    m = wbk * 4 + j
                    if m in (40, 41):
                        continue
                    ps = psr.next()
                    mm_group(pg, ps[:, :n], [(w[:, k, j * 128:(j + 1) * 128], h[:, k, :n]) for k in range(KC)],
                             reads=[w.b, h.b], writes=[ps.b])
                    if sg is None:
                        sg = stg.next()
                    if m >= 42:
                        pg.op("act", lambda e, ps=ps, sg=sg, j=j: e.activation(out=sg[:, j, :n], in_=ps[:, :n], func=AF.Sigmoid),
                              reads=[ps.b], writes=[sg.b])
                    elif 30 <= m < 40:
                        gi = 0 if m < 38 else 1
                        s2 = sqh.next()
                        pg.op("act", lambda e, ps=ps, s2=s2: e.activation(out=s2[:, :n], in_=ps[:, :n], func=AF.Square),
                              reads=[ps.b], writes=[s2.b])
                        p2 = psx.next()
                        mm_group(pg, p2[:, :n], [(g.ones[:, :], s2[:, :n])], reads=[s2.b, g.ones.b], writes=[p2.b])
                        rh = tmpb.next()
                        th = tmpb.next()
                        rstd_from_ss(g, p2, n, rh, th, float(HD))
                        if isctx:
                            pg.op("dve", lambda e, ps=ps, sg=sg, j=j, gi=gi, rh=rh: e.scalar_tensor_tensor(
                                out=sg[:, j, :n], in0=ps[:, :n], scalar=gains[:, gi:gi + 1], in1=rh[:, :n],
                                op0=ALU.mult, op1=ALU.mult), reads=[ps.b, rh.b, gains.b], writes=[sg.b])
                        else:
                            q = qn.next()
                            pg.op("dve", lambda e, ps=ps, q=q, gi=gi, rh=rh: e.scalar_tensor_tensor(
                                out=q[:, :n], in0=ps[:, :n], scalar=gains[:, gi:gi + 1], in1=rh[:, :n],
                                op0=ALU.mult, op1=ALU.mult), reads=[ps.b, rh.b, gains.b], writes=[q.b])
                            p3 = psx.next()
                            mm_group(pg, p3[:, :n], [(g.perm[:, :], q[:, :n])], reads=[q.b, g.perm.b], writes=[p3.b])
                            ta = tmpa.next()
                            tb = tmpa.next()
                            pg.op("dve", lambda e, q=q, ta=ta: e.tensor_tensor(out=ta[:, :n], in0=q[:, :n], in1=ct[:, :n], op=ALU.mult),
                                  reads=[q.b, ct.b], writes=[ta.b])
                            pg.op("dve", lambda e, p3=p3, tb=tb: e.tensor_tensor(out=tb[:, :n], in0=p3[:, :n], in1=sn[:, :n], op=ALU.mult),
                                  reads=[p3.b, sn.b], writes=[tb.b])
                            pg.op("dve", lambda e, ta=ta, tb=tb, sg=sg, j=j: e.tensor_tensor(out=sg[:, j, :n], in0=ta[:, :n], in1=tb[:, :n], op=ALU.add),
                                  reads=[ta.b, tb.b], writes=[sg.b])
                    else:
                        pg.op("act", lambda e, ps=ps, sg=sg, j=j: e.activation(out=sg[:, j, :n], in_=ps[:, :n], func=AF.Copy),
                              reads=[ps.b], writes=[sg.b])
                if wbk == 10:
                    dst = PRJ[c0 + 256:c0 + 512, n0:n0 + n].rearrange("(j p) t -> p j t", p=128)
                    pg.dma("pool", dst, sg[:, 2:4, :n], reads=[sg.b], writes=[g.db["PRJ"]])
                    vs = vst.next()
                    for ts in range(n // 128):
                        pv = psr.next()
                        mm_group(pg, pv[:, 0:256], [(h[:, k, ts * 128:(ts + 1) * 128], w[:, k, 0:256]) for k in range(KC)],
                                 reads=[w.b, h.b], writes=[pv.b])
                        pg.op("act", lambda e, pv=pv, ts=ts: e.activation(out=vs[:, ts, :], in_=pv[:, 0:256], func=AF.Copy),
                              reads=[pv.b], writes=[vs.b])
                    dstv = S["VTOK"][n0:n0 + n, :].rearrange("(s p) d -> p s d", p=128)
                    pg.dma("pool", dstv, vs[:, :n // 128, :], reads=[vs.b], writes=[g.db["VTOK"]])
                else:
                    nj = cw // 128
                    dst = PRJ[c0:c0 + cw, n0:n0 + n].rearrange("(j p) t -> p j t", p=128)
                    pg.dma("pool", dst, sg[:, :nj, :n], reads=[sg.b], writes=[g.db["PRJ"]])
        pg.barrier()


def emit_ssm(g, l):
    from contextlib import ExitStack
    pg, nc, I, S = g.pg, g.nc, g.I, g.S
    T = g.T
    C = T // 8
    CC = CTX // 8
    PRJ = S["PRJ"]
    nblk = (T + 511) // 512
    cblks = [(i * 512, min(512, T - i * 512)) for i in range(nblk)]
    with ExitStack() as st:
        PW = sb_tile(g, st, "ss_pw", [128, 96, NPOW], F32)
        QW = sb_tile(g, st, "ss_qw", [128, 96, NPOW], F32)
        BBT = sb_tile(g, st, "ss_bbt", [16, 96, 128], BF16)
        CTt = sb_tile(g, st, "ss_ct", [128, 96, 16], BF16)
        DS = sb_tile(g, st, "ss_ds", [16, NG, 16], BF16)
        with ExitStack() as st2:
            def disc_alloc(prefix, npart, ncol):
                tl = {}
                for nm in ("lre", "lim", "dt", "mag", "ang", "ki", "kf", "cs", "sn", "abr", "abi", "nr", "den", "fre", "fim", "t1", "t2"):
                    dt_ = I32 if nm == "ki" else F32
                    tl[nm] = sb_tile(g, st2, f"{prefix}_{nm}", [npart, ncol], dt_)
                return tl

            def disc(tl, lre_src, lim_src, ldt_src):
                pg.dma("sp", tl["lre"][:, :], lre_src, reads=[g.inbuf], writes=[tl["lre"].b])
                pg.dma("sp", tl["lim"][:, :], lim_src, reads=[g.inbuf], writes=[tl["lim"].b])
                pg.dma("sp", tl["dt"][:, :], ldt_src, reads=[g.inbuf], writes=[tl["dt"].b])

                def dv(fn, r, w):
                    pg.op("dve", fn, reads=[tl[x].b for x in r], writes=[tl[x].b for x in w])

                def ac(fn, r, w):
                    pg.op("act", fn, reads=[tl[x].b for x in r], writes=[tl[x].b for x in w])
                A = lambda nm: tl[nm][:, :]
                dv(lambda e: e.tensor_scalar_min(out=A("lre"), in0=A("lre"), scalar1=-1e-4), ["lre"], ["lre"])
                ac(lambda e: e.activation(out=A("dt"), in_=A("dt"), func=AF.Exp), ["dt"], ["dt"])
                dv(lambda e: e.tensor_tensor(out=A("t1"), in0=A("lre"), in1=A("dt"), op=ALU.mult), ["lre", "dt"], ["t1"])
                ac(lambda e: e.activation(out=A("mag"), in_=A("t1"), func=AF.Exp), ["t1"], ["mag"])
                dv(lambda e: e.tensor_tensor(out=A("ang"), in0=A("lim"), in1=A("dt"), op=ALU.mult), ["lim", "dt"], ["ang"])

                def sin_of(dst, shift):
                    dv(lambda e: e.tensor_scalar(out=A("t1"), in0=A("ang"), scalar1=shift, scalar2=None, op0=ALU.add), ["ang"], ["t1"])
                    dv(lambda e: e.tensor_scalar(out=A("ki"), in0=A("t1"), scalar1=1.0 / TWO_PI, scalar2=None, op0=ALU.mult), ["t1"], ["ki"])
                    dv(lambda e: e.tensor_copy(out=A("kf"), in_=A("ki")), ["ki"], ["kf"])
                    dv(lambda e: e.scalar_tensor_tensor(out=A("t2"), in0=A("kf"), scalar=-TWO_PI, in1=A("t1"), op0=ALU.mult, op1=ALU.add),
                       ["kf", "t1"], ["t2"])
                    dv(lambda e: e.tensor_scalar(out=A("t2"), in0=A("t2"), scalar1=math.pi, scalar2=-math.pi, op0=ALU.min, op1=ALU.max),
                       ["t2"], ["t2"])
                    ac(lambda e: e.activation(out=A(dst), in_=A("t2"), func=AF.Sin), ["t2"], [dst])
                sin_of("sn", 0.0)
                sin_of("cs", math.pi / 2)
                dv(lambda e: e.tensor_tensor(out=A("abr"), in0=A("mag"), in1=A("cs"), op=ALU.mult), ["mag", "cs"], ["abr"])
                dv(lambda e: e.tensor_tensor(out=A("abi"), in0=A("mag"), in1=A("sn"), op=ALU.mult), ["mag", "sn"], ["abi"])
                dv(lambda e: e.tensor_scalar(out=A("nr"), in0=A("abr"), scalar1=-1.0, scalar2=None, op0=ALU.add), ["abr"], ["nr"])
                dv(lambda e: e.tensor_tensor(out=A("t1"), in0=A("lre"), in1=A("lre"), op=ALU.mult), ["lre"], ["t1"])
                dv(lambda e: e.tensor_tensor(out=A("t2"), in0=A("lim"), in1=A("lim"), op=ALU.mult), ["lim"], ["t2"])
                dv(lambda e: e.tensor_tensor(out=A("den"), in0=A("t1"), in1=A("t2"), op=ALU.add), ["t1", "t2"], ["den"])
                dv(lambda e: e.reciprocal(out=A("den"), in_=A("den")), ["den"], ["den"])
                dv(lambda e: e.tensor_tensor(out=A("t1"), in0=A("nr"), in1=A("lre"), op=ALU.mult), ["nr", "lre"], ["t1"])
                dv(lambda e: e.tensor_tensor(out=A("t2"), in0=A("abi"), in1=A("lim"), op=ALU.mult), ["abi", "lim"], ["t2"])
                dv(lambda e: e.tensor_tensor(out=A("t1"), in0=A("t1"), in1=A("t2"), op=ALU.add), ["t1", "t2"], ["t1"])
                dv(lambda e: e.tensor_tensor(out=A("fre"), in0=A("t1"), in1=A("den"), op=ALU.mult), ["t1", "den"], ["fre"])
                dv(lambda e: e.tensor_tensor(out=A("t1"), in0=A("abi"), in1=A("lre"), op=ALU.mult), ["abi", "lre"], ["t1"])
                dv(lambda e: e.tensor_tensor(out=A("t2"), in0=A("nr"), in1=A("lim"), op=ALU.mult), ["nr", "lim"], ["t2"])
                dv(lambda e: e.tensor_tensor(out=A("t1"), in0=A("t1"), in1=A("t2"), op=ALU.subtract), ["t1", "t2"], ["t1"])
                dv(lambda e: e.tensor_tensor(out=A("fim"), in0=A("t1"), in1=A("den"), op=ALU.mult), ["t1", "den"], ["fim"])
            ta = disc_alloc("da", 128, 96)
            disc(ta, I["lam_re_a"][l], I["lam_im_a"][l], I["log_dt_a"][l])
            pa = sb_tile(g, st2, "da_p", [128, 96], F32)
            pb = sb_tile(g, st2, "da_q", [128, 96], F32)

            def cmul(ore, oim, are, aim, bre, bim, rd, wr):
                pg.op("dve", lambda e: e.tensor_tensor(out=pa[:, :], in0=are, in1=bre, op=ALU.mult), reads=rd, writes=[pa.b])
                pg.op("dve", lambda e: e.tensor_tensor(out=pb[:, :], in0=aim, in1=bim, op=ALU.mult), reads=rd, writes=[pb.b])
                pg.op("dve", lambda e: e.tensor_tensor(out=ore, in0=pa[:, :], in1=pb[:, :], op=ALU.subtract), reads=[pa.b, pb.b], writes=wr)
                pg.op("dve", lambda e: e.tensor_tensor(out=pa[:, :], in0=are, in1=bim, op=ALU.mult), reads=rd, writes=[pa.b])
                pg.op("dve", lambda e: e.tensor_tensor(out=pb[:, :], in0=aim, in1=bre, op=ALU.mult), reads=rd, writes=[pb.b])
                pg.op("dve", lambda e: e.tensor_tensor(out=oim, in0=pa[:, :], in1=pb[:, :], op=ALU.add), reads=[pa.b, pb.b], writes=wr)
            pg.op("dve", lambda e: e.tensor_copy(out=PW[:, :, 0], in_=ta["abr"][:, :]), reads=[ta["abr"].b], writes=[PW.b])
            pg.op("dve", lambda e: e.tensor_copy(out=QW[:, :, 0], in_=ta["abi"][:, :]), reads=[ta["abi"].b], writes=[QW.b])
            for j in range(1, 7):
                cmul(PW[:, :, j], QW[:, :, j], PW[:, :, j - 1], QW[:, :, j - 1], PW[:, :, 0], QW[:, :, 0], [PW.b, QW.b], [PW.b, QW.b])
            cmul(PW[:, :, 7], QW[:, :, 7], PW[:, :, 3], QW[:, :, 3], PW[:, :, 3], QW[:, :, 3], [PW.b, QW.b], [PW.b, QW.b])
            for j in range(8, NPOW):
                cmul(PW[:, :, j], QW[:, :, j], PW[:, :, j - 1], QW[:, :, j - 1], PW[:, :, j - 1], QW[:, :, j - 1], [PW.b, QW.b], [PW.b, QW.b])
            CH = 12 * NST
            tb_ = disc_alloc("db", 16, CH)
            bre = sb_tile(g, st2, "db_bre", [16, CH], F32)
            bim = sb_tile(g, st2, "db_bim", [16, CH], F32)
            u1 = sb_tile(g, st2, "db_u1", [16, CH], F32)
            u2 = sb_tile(g, st2, "db_u2", [16, CH], F32)
            fre, fim = tb_["fre"], tb_["fim"]
            v3 = lambda t: t[:, :].rearrange("q (a n) -> q a n", n=NST)
            for ch in range(8):
                cs_ = slice(ch * CH, (ch + 1) * CH)
                ds_ = slice(ch * 12, (ch + 1) * 12)
                disc(tb_, I["lam_re_b"][l][:, cs_], I["lam_im_b"][l][:, cs_], I["log_dt_b"][l][:, cs_])
                pg.dma("sp", bre[:, :], I["b_re_b"][l][:, cs_], reads=[g.inbuf], writes=[bre.b])
                pg.dma("sp", bim[:, :], I["b_im_b"][l][:, cs_], reads=[g.inbuf], writes=[bim.b])
                pg.op("dve", lambda e: e.tensor_tensor(out=u1[:, :], in0=fre[:, :], in1=bre[:, :], op=ALU.mult), reads=[fre.b, bre.b], writes=[u1.b])
                pg.op("dve", lambda e: e.tensor_tensor(out=u2[:, :], in0=fim[:, :], in1=bim[:, :], op=ALU.mult), reads=[fim.b, bim.b], writes=[u2.b])
                pg.op("dve", lambda e: e.tensor_tensor(out=BBT[:, ds_, 0:NST], in0=v3(u1), in1=v3(u2), op=ALU.subtract), reads=[u1.b, u2.b], writes=[BBT.b])
                pg.op("dve", lambda e: e.tensor_tensor(out=u1[:, :], in0=fre[:, :], in1=bim[:, :], op=ALU.mult), reads=[fre.b, bim.b], writes=[u1.b])
                pg.op("dve", lambda e: e.tensor_tensor(out=u2[:, :], in0=fim[:, :], in1=bre[:, :], op=ALU.mult), reads=[fim.b, bre.b], writes=[u2.b])
                pg.op("dve", lambda e: e.tensor_tensor(out=BBT[:, ds_, NST:2 * NST], in0=v3(u1), in1=v3(u2), op=ALU.add), reads=[u1.b, u2.b], writes=[BBT.b])
            cri = sb_tile(g, st2, "db_cri", [128, 96 * 16], F32)
            pg.dma("sp", cri[:, :], I["c_ri"][l], reads=[g.inbuf], writes=[cri.b])
            pg.op("act", lambda e: e.activation(out=CTt[0:64, :, :], in_=cri[0:64, :].rearrange("p (a q) -> p a q", q=16), func=AF.Copy),
                  reads=[cri.b], writes=[CTt.b])
            pg.op("act", lambda e: e.activation(out=CTt[64:128, :, :], in_=cri[64:128, :].rearrange("p (a q) -> p a q", q=16), func=AF.Copy, scale=-1.0),
                  reads=[cri.b], writes=[CTt.b])
            dsk = sb_tile(g, st2, "db_dsk", [16, NG], F32)
            pg.dma("sp", dsk[:, :], I["ssm_dT"][l], reads=[g.inbuf], writes=[dsk.b])
            for gi in range(NG):
                pg.op("dve", lambda e, gi=gi: e.tensor_scalar(out=DS[:, gi, :], in0=g.ident_f[0:16, 0:16], scalar1=dsk[:, gi:gi + 1],
                                                               scalar2=None, op0=ALU.mult), reads=[g.ident_f.b, dsk.b], writes=[DS.b])
            pg.barrier()
        Vp = sb_rot(g, st, "ss_v", [128, T], BF16, 4, nb=nblk)
        Mp = sb_rot(g, st, "ss_m", [128, NPOW, 128], BF16, 4)
        Up = sb_rot(g, st, "ss_u", [16, T], BF16, 2)
        Zp = sb_rot(g, st, "ss_z", [128, C], BF16, 4)
        mtmp = sb_rot(g, st, "ss_mt", [128, 128], F32, 3)
        yst = sb_rot(g, st, "ss_y", [16, 512], F32, 2)
        ya = sb_rot(g, st, "ss_ya", [16, 512], F32, 2)
        yb = sb_rot(g, st, "ss_yb", [16, 512], F32, 2)
        yo = sb_rot(g, st, "ss_yo", [16, 512], BF16, 2)
        psr = Rot(g.psum)
        for gi in range(NG):
            u = Up.next()
            pg.dma("sp", u[:, :], PRJ[OFF_U + 16 * gi:OFF_U + 16 * gi + 16, :], reads=[g.db["PRJ"]], writes=[u.b])
            Vd = []
            Md = []
            Zd = []
            for d in range(2):
                dg = d * NG + gi
                V = Vp.next()
                M = Mp.next()
                Z = Zp.next()
                Vd.append(V)
                Md.append(M)
                Zd.append(Z)
                for j in range(NPOW):
                    mt = mtmp.next()
                    pg.op("act", lambda e, mt=mt, j=j, dg=dg: e.activation(out=mt[:, :], in_=g.jmat_f[:, :], func=AF.Identity, scale=QW[:, dg, j:j + 1]),
                          reads=[g.jmat_f.b, QW.b], writes=[mt.b])
                    pg.op("dve", lambda e, mt=mt, j=j, dg=dg, M=M: e.scalar_tensor_tensor(
                        out=M[:, j, :], in0=g.ident_f[:, :], scalar=PW[:, dg, j:j + 1], in1=mt[:, :], op0=ALU.mult, op1=ALU.add),
                        reads=[g.ident_f.b, PW.b, mt.b], writes=[M.b])
            for bi, (c0, cn) in enumerate(cblks):
                for d in range(2):
                    dg = d * NG + gi
                    ps = psr.next()
                    mm_group(pg, ps[:, :cn], [(BBT[:, dg, :], u[:, c0:c0 + cn])], reads=[BBT.b, u.b], writes=[ps.b])
                    V = Vd[d]
                    pg.op("act", lambda e, ps=ps, V=V, c0=c0, cn=cn: e.activation(out=V[:, c0:c0 + cn], in_=ps[:, :cn], func=AF.Copy),
                          reads=[ps.b], writes=[V.bufs[bi]])
            for sh in (1, 2, 4):
                pj = POW_IDX[sh]
                for bi, (c0, cn) in enumerate(cblks):
                    for d in range(2):
                        V = Vd[d]
                        M = Md[d]
                        V3 = V[:, c0:c0 + cn].rearrange("p (c t) -> p c t", t=8)
                        ps = psr.next()
                        P3 = ps[:, :cn].rearrange("p (c t) -> p c t", t=8)
                        if d == 0:
                            src, dst, pdst = V3[:, :, 0:8 - sh], V3[:, :, sh:8], P3[:, :, 0:8 - sh]
                        else:
                            src, dst, pdst = V3[:, :, sh:8], V3[:, :, 0:8 - sh], P3[:, :, 0:8 - sh]
                        mm_group(pg, pdst, [(M[:, pj, :], src)], reads=[M.b, V.bufs[bi]], writes=[ps.b])
                        pg.op("dve", lambda e, dst=dst, pdst=pdst: e.tensor_tensor(out=dst, in0=dst, in1=pdst, op=ALU.add),
                              reads=[ps.b, V.bufs[bi]], writes=[V.bufs[bi]])
            for d in range(2):
                V = Vd[d]
                Z = Zd[d]
                V3 = V[:, :].rearrange("p (c t) -> p c t", t=8)
                if d == 0:
                    pg.op("dve", lambda e, V3=V3, Z=Z: e.tensor_copy(out=Z[:, :], in_=V3[:, :, 7]), reads=V.bufs, writes=[Z.b])
                else:
                    pg.op("dve", lambda e, V3=V3, Z=Z: e.tensor_copy(out=Z[:, 0:C - CC], in_=V3[:, CC:C, 0]), reads=V.bufs, writes=[Z.b])
                    pg.op("dve", lambda e, V3=V3, Z=Z: e.tensor_copy(out=Z[:, C - CC:C], in_=V3[:, 0:CC, 0]), reads=V.bufs, writes=[Z.b])
            i = 0
            while (1 << i) < C:
                sh = 1 << i
                pj = POW_IDX[8 * sh]
                n = C - sh
                pieces = [(o, min(512, n - o)) for o in range(0, n, 512)]
                pss = {}
                for d in range(2):
                    Z = Zd[d]
                    M = Md[d]
                    for (o, pn) in pieces:
                        ps = psr.next()
                        pss[(d, o)] = ps
                        src = Z[:, o:o + pn] if d == 0 else Z[:, sh + o:sh + o + pn]
                        mm_group(pg, ps[:, :pn], [(M[:, pj, :], src)], reads=[M.b, Z.b], writes=[ps.b])
                for d in range(2):
                    Z = Zd[d]
                    for (o, pn) in pieces:
                        ps = pss[(d, o)]
                        dst = Z[:, sh + o:sh + o + pn] if d == 0 else Z[:, o:o + pn]
                        pg.op("dve", lambda e, dst=dst, ps=ps, pn=pn: e.tensor_tensor(out=dst, in0=dst, in1=ps[:, :pn], op=ALU.add),
                              reads=[ps.b, Z.b], writes=[Z.b])
                i += 1
            for d in range(2):
                V = Vd[d]
                Z = Zd[d]
                M = Md[d]
                V3 = V[:, :].rearrange("p (c t) -> p c t", t=8)
                if d == 0:
                    pg.op("dve", lambda e, V3=V3, Z=Z: e.tensor_copy(out=V3[:, :, 7], in_=Z[:, :]), reads=[Z.b], writes=V.bufs)
                else:
                    pg.op("dve", lambda e, V3=V3, Z=Z: e.tensor_copy(out=V3[:, CC:C, 0], in_=Z[:, 0:C - CC]), reads=[Z.b], writes=V.bufs)
                    pg.op("dve", lambda e, V3=V3, Z=Z: e.tensor_copy(out=V3[:, 0:CC, 0], in_=Z[:, C - CC:C]), reads=[Z.b], writes=V.bufs)
            for tau in range(7):
                for d in range(2):
                    V = Vd[d]
                    Z = Zd[d]
                    M = Md[d]
                    V3 = V[:, :].rearrange("p (c t) -> p c t", t=8)
                    if d == 0:
                        pj = POW_IDX[tau + 1]
                        segs = [(1, 0, C - 1)]
                        col = tau
                    else:
                        t_ = tau + 1
                        pj = POW_IDX[8 - t_]
                        col = t_
                        segs = [(CC, 1, C - CC), (0, C - CC + 1, CC - 1)]
                    for (dc0, z0, cnt) in segs:
                        for o in range(0, cnt, 512):
                            pn = min(512, cnt - o)
                            ps = psr.next()
                            mm_group(pg, ps[:, :pn], [(M[:, pj, :], Z[:, z0 + o:z0 + o + pn])], reads=[M.b, Z.b], writes=[ps.b])
                            dst = V3[:, dc0 + o:dc0 + o + pn, col]
                            pg.op("dve", lambda e, dst=dst, ps=ps, pn=pn: e.tensor_tensor(out=dst, in0=dst, in1=ps[:, :pn], op=ALU.add),
                                  reads=[ps.b] + V.bufs, writes=V.bufs)
            for bi, (c0, cn) in enumerate(cblks):
                ps = psr.next()
                pairs = [(DS[:, gi, :], u[:, c0:c0 + cn]),
                         (CTt[:, gi, :], Vd[0][:, c0:c0 + cn]),
                         (CTt[:, NG + gi, :], Vd[1][:, c0:c0 + cn])]
                mm_group(pg, ps[0:16, :cn], pairs, reads=[DS.b, CTt.b, u.b, Vd[0].bufs[bi], Vd[1].bufs[bi]], writes=[ps.b])
                y = yst.next()
                a_ = ya.next()
                b_ = yb.next()
                o_ = yo.next()
                pg.op("act", lambda e, ps=ps, y=y, cn=cn: e.activation(out=y[:, :cn], in_=ps[0:16, :cn], func=AF.Copy), reads=[ps.b], writes=[y.b])
                pg.op("act", lambda e, y=y, a_=a_, cn=cn: e.activation(out=a_[:, :cn], in_=y[:, :cn], func=AF.Square), reads=[y.b], writes=[a_.b])
                pg.op("dve", lambda e, y=y, a_=a_, b_=b_, cn=cn: e.scalar_tensor_tensor(out=b_[:, :cn], in0=a_[:, :cn], scalar=0.044715, in1=y[:, :cn],
                                                                                        op0=ALU.mult, op1=ALU.mult), reads=[a_.b, y.b], writes=[b_.b])
                pg.op("dve", lambda e, y=y, b_=b_, cn=cn: e.tensor_tensor(out=b_[:, :cn], in0=b_[:, :cn], in1=y[:, :cn], op=ALU.add), reads=[b_.b, y.b], writes=[b_.b])
                pg.op("act", lambda e, a_=a_, b_=b_, cn=cn: e.activation(out=a_[:, :cn], in_=b_[:, :cn], func=AF.Sigmoid, scale=2.0 * math.sqrt(2.0 / math.pi)),
                      reads=[b_.b], writes=[a_.b])
                pg.op("dve", lambda e, y=y, a_=a_, o_=o_, cn=cn: e.tensor_tensor(out=o_[:, :cn], in0=a_[:, :cn], in1=y[:, :cn], op=ALU.mult), reads=[a_.b, y.b], writes=[o_.b])
                pg.dma("pool", S["YS"][16 * gi:16 * gi + 16, c0:c0 + cn], o_[:, :cn], reads=[o_.b], writes=[g.db["YS"]])
        pg.barrier()


def emit_attn(g, l):
    from contextlib import ExitStack
    pg, nc, I, S = g.pg, g.nc, g.I, g.S
    T = g.T
    NKT = T // 128
    PRJ = S["PRJ"]
    with ExitStack() as st:
        KTp = sb_rot(g, st, "at_k", [128, T], BF16, 2)
        VAp = sb_rot(g, st, "at_v", [128, NKT, 130], BF16, 2)
        Qp = sb_rot(g, st, "at_q", [128, 512], BF16, 3)
        Ep = sb_rot(g, st, "at_e", [128, 512], BF16, 4)
        Onp = sb_rot(g, st, "at_on", [128, 128], BF16, 4)
        recp = sb_rot(g, st, "at_rec", [128, 1], F32, 4)
        atp = sb_rot(g, st, "at_o", [128, 512], BF16, 3)
        psS = Rot(g.psum[0:3])
        psO = Rot([(g.psum[3], g.psum[4]), (g.psum[5], g.psum[6])])
        psT = g.psum[7]
        for hk in range(NKV):
            Kt = KTp.next()
            Va = VAp.next()
            pg.dma("sp", Kt[:, :], PRJ[OFF_K + hk * 128:OFF_K + (hk + 1) * 128, :], reads=[g.db["PRJ"]], writes=[Kt.b])
            pg.dma("sp", Va[:, :, 0:128], S["VTOK"][:, hk * 128:(hk + 1) * 128].rearrange("(kt p) d -> p kt d", p=128),
                   reads=[g.db["VTOK"]], writes=[Va.b])
            pg.op("dve", lambda e, Va=Va: e.memset(Va[:, :, 128:130], 1.0), writes=[Va.b])
            for gq in range(4):
                h = hk * 4 + gq
                for (n0, n, isctx) in g.blocks:
                    kts = list(range(CTX // 128)) if isctx else list(range(NKT))
                    q = Qp.next()
                    pg.dma("sp", q[:, :n], PRJ[OFF_Q + h * 128:OFF_Q + (h + 1) * 128, n0:n0 + n], reads=[g.db["PRJ"]], writes=[q.b])
                    Ob = psO.next()
                    nqs = n // 128

                    def oacc(qs):
                        t = Ob[qs // 2]
                        c = (qs % 2) * 130
                        return t, t[:, c:c + 129]

                    def s_mm(kt):
                        ps = psS.next()
                        mm_group(pg, ps[:, :n], [(Kt[:, kt * 128:(kt + 1) * 128], q[:, :n])], reads=[Kt.b, q.b], writes=[ps.b])
                        return ps
                    ps_next = s_mm(kts[0])
                    for idx, kt in enumerate(kts):
                        ps = ps_next
                        if idx + 1 < len(kts):
                            ps_next = s_mm(kts[idx + 1])
                        E = Ep.next()
                        pg.op("act", lambda e, ps=ps, E=E: e.activation(out=E[:, :n], in_=ps[:, :n], func=AF.Exp, scale=ATTN_SCALE),
                              reads=[ps.b], writes=[E.b])
                        for qs in range(nqs):
                            t, oap = oacc(qs)
                            mm_group(pg, oap, [(E[:, qs * 128:(qs + 1) * 128], Va[:, kt, 0:129])], reads=[E.b, Va.b], writes=[t.b],
                                     start=(idx == 0), stop=(idx == len(kts) - 1))
                    ao = atp.next()
                    for qs in range(nqs):
                        t, oap = oacc(qs)
                        rc = recp.next()
                        on = Onp.next()
                        pg.op("dve", lambda e, rc=rc, oap=oap: e.reciprocal(out=rc[:, :], in_=oap[:, 128:129]), reads=[t.b], writes=[rc.b])
                        pg.op("dve", lambda e, rc=rc, oap=oap, on=on: e.tensor_scalar(out=on[:, :], in0=oap[:, 0:128], scalar1=rc[:, 0:1], scalar2=None,
                                                                                       op0=ALU.mult), reads=[t.b, rc.b], writes=[on.b])
                        mm_group(pg, psT[:, qs * 128:(qs + 1) * 128], [(on[:, :], g.ident[:, :])], reads=[on.b, g.ident.b], writes=[psT.b])
                    pg.op("act", lambda e, ao=ao: e.activation(out=ao[:, :n], in_=psT[:, :n], func=AF.Copy), reads=[psT.b], writes=[ao.b])
                    pg.dma("pool", S["AT"][h * 128:(h + 1) * 128, n0:n0 + n], ao[:, :n], reads=[ao.b], writes=[g.db["AT"]])
        pg.barrier()


def emit_merge(g, l):
    from contextlib import ExitStack
    pg, nc, I, S = g.pg, g.nc, g.I, g.S
    T = g.T
    PRJ = S["PRJ"]
    PRJ3 = PRJ.rearrange("(k p) t -> p k t", p=128)
    GV = PRJ[OFF_G:IN_W, :].rearrange("(br m p) t -> p br m t", br=3, m=16, p=128)
    YS3 = S["YS"].rearrange("(k p) t -> p k t", p=128)
    AT3 = S["AT"].rearrange("(k p) t -> p k t", p=128)
    MRG3 = S["MRG"].rearrange("(k p) t -> p k t", p=128)
    WCO = S["WCO"][l].rearrange("(k p) n -> p k n", p=128)
    WGL = S["WGL"][l].rearrange("(k p) n -> p k n", p=128)
    WAO = S["WAO"][l].rearrange("(k p) n -> p k n", p=128)
    wb = [g.wbuf[("WCO", l)], g.wbuf[("WGL", l)], g.wbuf[("WAO", l)]]
    with ExitStack() as st:
        cw = sb_tile(g, st, "mg_cw", [128, 8, 3], F32)
        DW = sb_tile(g, st, "mg_dw", [128, 8, 3, 128], BF16)
        pg.dma("sp", cw[:, :, :], I["conv_w"][l], reads=[g.inbuf], writes=[cw.b])
        for k in range(8):
            for j in range(3):
                pg.op("dve", lambda e, k=k, j=j: e.tensor_scalar(out=DW[:, k, j, :], in0=g.ident_f[:, :], scalar1=cw[:, k, j:j + 1], scalar2=None,
                                                                 op0=ALU.mult), reads=[g.ident_f.b, cw.b], writes=[DW.b])
        gbp = sb_rot(g, st, "mg_gb", [128, 8, 512], BF16, 1)
        gcp = sb_rot(g, st, "mg_gc", [128, 8, 514], BF16, 1)
        vp = sb_rot(g, st, "mg_v", [128, 8, 514], BF16, 1)
        up = sb_rot(g, st, "mg_u", [128, 8, 514], BF16, 2)
        zp = sb_rot(g, st, "mg_z", [128, 8, 512], BF16, 1)
        ysp = sb_rot(g, st, "mg_ys", [128, 6, 512], BF16, 1)
        atp = sb_rot(g, st, "mg_at", [128, 8, 512], BF16, 1)
        wtp = sb_rot(g, st, "mg_w", [128, 28, 512], BF16, 2)
        gtp = sb_rot(g, st, "mg_g", [128, 3, 4, 512], BF16, 2)
        sgp = sb_rot(g, st, "mg_sg", [128, 512], F32, 2)
        t1p = sb_rot(g, st, "mg_t1", [128, 512], F32, 2)
        t2p = sb_rot(g, st, "mg_t2", [128, 512], F32, 2)
        t3p = sb_rot(g, st, "mg_t3", [128, 512], F32, 2)
        mop = sb_rot(g, st, "mg_mo", [128, 4, 512], BF16, 2)
        psr = Rot(g.psum)
        for (n0, n, isctx) in g.blocks:
            seq0 = 0 if isctx else CTX
            seq1 = CTX if isctx else T
            gb = gbp.next()
            gc = gcp.next()
            v = vp.next()
            lo = max(n0 - 1, seq0)
            hi = min(n0 + n + 1, seq1)
            o0 = lo - (n0 - 1)
            pg.dma("sp", gb[:, :, :n], PRJ3[:, 0:8, n0:n0 + n], reads=[g.db["PRJ"]], writes=[gb.b])
            pg.dma("sp", gc[:, :, o0:o0 + hi - lo], PRJ3[:, 8:16, lo:hi], reads=[g.db["PRJ"]], writes=[gc.b])
            pg.dma("sp", v[:, :, o0:o0 + hi - lo], PRJ3[:, 16:24, lo:hi], reads=[g.db["PRJ"]], writes=[v.b])
            u = up.next()
            pg.op("dve", lambda e, u=u, gc=gc, v=v, o0=o0, hi=hi, lo=lo: e.tensor_tensor(
                out=u[:, :, o0:o0 + hi - lo], in0=gc[:, :, o0:o0 + hi - lo], in1=v[:, :, o0:o0 + hi - lo], op=ALU.mult),
                reads=[gc.b, v.b], writes=[u.b])
            if o0 == 1:
                pg.op("dve", lambda e, u=u: e.memset(u[:, :, 0:1], 0.0), writes=[u.b])
            if hi - lo + o0 < n + 2:
                pg.op("dve", lambda e, u=u: e.memset(u[:, :, n + 1:n + 2], 0.0), writes=[u.b])
            z = zp.next()
            for k in range(8):
                ps = psr.next()
                mm_group(pg, ps[:, :n], [(DW[:, k, j, :], u[:, k, j:j + n]) for j in range(3)], reads=[DW.b, u.b], writes=[ps.b])
                pg.op("dve", lambda e, ps=ps, k=k, z=z, gb=gb: e.tensor_tensor(out=z[:, k, :n], in0=ps[:, :n], in1=gb[:, k, :n], op=ALU.mult),
                      reads=[ps.b, gb.b], writes=[z.b])
            ys = ysp.next()
            at = atp.next()
            pg.dma("sp", ys[:, :, :n], YS3[:, :, n0:n0 + n], reads=[g.db["YS"]], writes=[ys.b])
            pg.dma("sp", at[:, :, :n], AT3[:, :, n0:n0 + n], reads=[g.db["AT"]], writes=[at.b])
            for wbk in range(4):
                c0 = wbk * 512
                w = wtp.next()
                pg.dma("sp", w[:, 0:8, :], WCO[:, :, c0:c0 + 512], reads=[wb[0]], writes=[w.b])
                pg.dma("sp", w[:, 8:14, :], WGL[:, :, c0:c0 + 512], reads=[wb[1]], writes=[w.b])
                pg.dma("sp", w[:, 14:20, :], WGL[:, :, D + c0:D + c0 + 512], reads=[wb[1]], writes=[w.b])
                pg.dma("sp", w[:, 20:28, :], WAO[:, :, c0:c0 + 512], reads=[wb[2]], writes=[w.b])
                gt = gtp.next()
                for br in range(3):
                    pg.dma("sp", gt[:, br, :, :n], GV[:, br, wbk * 4:(wbk + 1) * 4, n0:n0 + n], reads=[g.db["PRJ"]], writes=[gt.b])
                mo = mop.next()
                for j in range(4):
                    cs = slice(j * 128, (j + 1) * 128)
                    pc = psr.next()
                    mm_group(pg, pc[:, :n], [(w[:, k, cs], z[:, k, :n]) for k in range(8)], reads=[w.b, z.b], writes=[pc.b])
                    pa = psr.next()
                    mm_group(pg, pa[:, :n], [(w[:, 8 + k, cs], ys[:, k, :n]) for k in range(6)], reads=[w.b, ys.b], writes=[pa.b])
                    pgt = psr.next()
                    mm_group(pg, pgt[:, :n], [(w[:, 14 + k, cs], ys[:, k, :n]) for k in range(6)], reads=[w.b, ys.b], writes=[pgt.b])
                    pt = psr.next()
                    mm_group(pg, pt[:, :n], [(w[:, 20 + k, cs], at[:, k, :n]) for k in range(8)], reads=[w.b, at.b], writes=[pt.b])
                    sg = sgp.next()
                    t1 = t1p.next()
                    t2 = t2p.next()
                    t3 = t3p.next()
                    pg.op("act", lambda e, pgt=pgt, sg=sg: e.activation(out=sg[:, :n], in_=pgt[:, :n], func=AF.Sigmoid), reads=[pgt.b], writes=[sg.b])
                    pg.op("dve", lambda e, pa=pa, sg=sg, t1=t1: e.tensor_tensor(out=t1[:, :n], in0=pa[:, :n], in1=sg[:, :n], op=ALU.mult),
                          reads=[pa.b, sg.b], writes=[t1.b])
                    pg.op("dve", lambda e, t1=t1, gt=gt, j=j: e.tensor_tensor(out=t1[:, :n], in0=t1[:, :n], in1=gt[:, 1, j, :n], op=ALU.mult),
                          reads=[t1.b, gt.b], writes=[t1.b])
                    pg.op("dve", lambda e, pc=pc, t2=t2, gt=gt, j=j: e.tensor_tensor(out=t2[:, :n], in0=pc[:, :n], in1=gt[:, 0, j, :n], op=ALU.mult),
                          reads=[pc.b, gt.b], writes=[t2.b])
                    pg.op("dve", lambda e, pt=pt, t3=t3, gt=gt, j=j: e.tensor_tensor(out=t3[:, :n], in0=pt[:, :n], in1=gt[:, 2, j, :n], op=ALU.mult),
                          reads=[pt.b, gt.b], writes=[t3.b])
                    pg.op("dve", lambda e, t1=t1, t2=t2: e.tensor_tensor(out=t1[:, :n], in0=t1[:, :n], in1=t2[:, :n], op=ALU.add),
                          reads=[t1.b, t2.b], writes=[t1.b])
                    pg.op("dve", lambda e, t1=t1, t3=t3, mo=mo, j=j: e.tensor_tensor(out=mo[:, j, :n], in0=t1[:, :n], in1=t3[:, :n], op=ALU.add),
                          reads=[t1.b, t3.b], writes=[mo.b])
                pg.dma("pool", MRG3[:, wbk * 4:(wbk + 1) * 4, n0:n0 + n], mo[:, :, :n], reads=[mo.b], writes=[g.db["MRG"]])
        pg.barrier()


def emit_wout(g, l, xin_name):
    from contextlib import ExitStack
    pg, nc, I, S = g.pg, g.nc, g.I, g.S
    T = g.T
    XIN = xin_ap(g, xin_name).rearrange("(k p) t -> p k t", p=128)
    xb = xin_buf(g, xin_name)
    MRG3 = S["MRG"].rearrange("(k p) t -> p k t", p=128)
    XM3 = S["XM"].rearrange("(k p) t -> p k t", p=128)
    WO = S["WO"][l].rearrange("(k p) n -> p k n", p=128)
    wbuf = g.wbuf[("WO", l)]
    cf6 = g.coef[l]
    with ExitStack() as st:
        rp = sb_rot(g, st, "wo_r", [128, KC, 512], BF16, 2)
        wtp = sb_rot(g, st, "wo_w", [128, KC, 512], BF16, 2)
        mx = sb_tile(g, st, "wo_mx", [128, KC, 512], F32)
        sqp = sb_rot(g, st, "wo_sq", [128, 512], BF16, 3)
        rstd = sb_rot(g, st, "wo_rstd", [128, 512], F32, 2)
        tmp = sb_rot(g, st, "wo_tmp", [128, 512], F32, 2)
        xk = sb_rot(g, st, "wo_xk", [128, 512], F32, 3)
        tk = sb_rot(g, st, "wo_tk", [128, 512], F32, 3)
        ok = sb_rot(g, st, "wo_ok", [128, 512], F32, 3)
        psr = Rot(g.psum[0:6])
        pss = Rot(g.psum[6:8])
        for (n0, n, isctx) in g.blocks:
            r = rp.next()
            pg.dma("sp", r[:, :, :n], MRG3[:, :, n0:n0 + n], reads=[g.db["MRG"]], writes=[r.b])
            ss = pss.next()
            for wbk in range(4):
                w = wtp.next()
                pg.dma("sp", w[:, :, :], WO[:, :, wbk * 512:(wbk + 1) * 512], reads=[wbuf], writes=[w.b])
                for j in range(4):
                    m = wbk * 4 + j
                    ps = psr.next()
                    mm_group(pg, ps[:, :n], [(w[:, k, j * 128:(j + 1) * 128], r[:, k, :n]) for k in range(KC)], reads=[w.b, r.b], writes=[ps.b])
                    pg.op("act", lambda e, ps=ps, m=m: e.activation(out=mx[:, m, :n], in_=ps[:, :n], func=AF.Copy), reads=[ps.b], writes=[mx.b])
                    sq = sqp.next()
                    pg.op("act", lambda e, ps=ps, sq=sq: e.activation(out=sq[:, :n], in_=ps[:, :n], func=AF.Square), reads=[ps.b], writes=[sq.b])
                    mm_group(pg, ss[:, :n], [(g.ones[:, :], sq[:, :n])], reads=[sq.b, g.ones.b], writes=[ss.b], start=(m == 0), stop=(m == 15))
            rs = rstd.next()
            t0 = tmp.next()
            rstd_from_ss(g, ss, n, rs, t0, float(D))
            for k in range(KC):
                x = xk.next()
                pg.dma("sp", x[:, :n], XIN[:, k, n0:n0 + n], reads=[xb], writes=[x.b])
                t = tk.next()
                o = ok.next()
                pg.op("dve", lambda e, k=k, t=t: e.scalar_tensor_tensor(out=t[:, :n], in0=mx[:, k, :n], scalar=cf6[:, 2, k, isctx:isctx + 1], in1=rs[:, :n],
                                                                         op0=ALU.mult, op1=ALU.mult), reads=[mx.b, rs.b, cf6.b], writes=[t.b])
                pg.op("dve", lambda e, t=t, x=x, o=o: e.tensor_tensor(out=o[:, :n], in0=t[:, :n], in1=x[:, :n], op=ALU.add), reads=[t.b, x.b], writes=[o.b])
                pg.dma("pool", XM3[:, k, n0:n0 + n], o[:, :n], reads=[o.b], writes=[g.db["XM"]])
        pg.barrier()


def emit_mlp(g, l, xout_name, last):
    from contextlib import ExitStack
    pg, nc, I, S = g.pg, g.nc, g.I, g.S
    T = g.T
    XM3 = S["XM"].rearrange("(k p) t -> p k t", p=128)
    XO3 = S[xout_name].rearrange("(k p) t -> p k t", p=128)
    OUT3 = g.out.rearrange("(k p) t -> p k t", p=128)
    WUP = S["WUP"][l].rearrange("(k p) n -> p k n", p=128)
    WDN = S["WDN"][l].rearrange("(kg k p) n -> p kg k n", p=128, k=KC)
    wu, wd = g.wbuf[("WUP", l)], g.wbuf[("WDN", l)]
    cf6 = g.coef[l]
    with ExitStack() as st:
        h2 = sb_tile(g, st, "ml_h2", [128, KC, 512], BF16)
        h1 = sb_tile(g, st, "ml_h1", [128, 64, 512], BF16)
        mo = sb_tile(g, st, "ml_mo", [128, KC, 512], F32)
        wtp = sb_rot(g, st, "ml_w", [128, KC, 512], BF16, 2)
        sqp = sb_rot(g, st, "ml_sq", [128, 512], BF16, 3)
        rstd = sb_rot(g, st, "ml_rstd", [128, 512], F32, 2)
        tmp = sb_rot(g, st, "ml_tmp", [128, 512], F32, 3)
        xk = sb_rot(g, st, "ml_xk", [128, 512], F32, 3)
        rl = sb_rot(g, st, "ml_rl", [128, 512], BF16, 3)
        ok = sb_rot(g, st, "ml_ok", [128, 512], F32, 3)
        psr = Rot(g.psum[0:6])
        pss = Rot(g.psum[6:8])
        for (n0, n, isctx) in g.blocks:
            if last and isctx:
                continue
            ss = pss.next()
            for k in range(KC):
                x = xk.next()
                pg.dma("sp", x[:, :n], XM3[:, k, n0:n0 + n], reads=[g.db["XM"]], writes=[x.b])
                sq = sqp.next()
                pg.op("act", lambda e, x=x, sq=sq: e.activation(out=sq[:, :n], in_=x[:, :n], func=AF.Square), reads=[x.b], writes=[sq.b])
                mm_group(pg, ss[:, :n], [(g.ones[:, :], sq[:, :n])], reads=[sq.b, g.ones.b], writes=[ss.b], start=(k == 0), stop=(k == KC - 1))
            rs = rstd.next()
            t0 = tmp.next()
            rstd_from_ss(g, ss, n, rs, t0, float(D))
            for k in range(KC):
                x = xk.next()
                pg.dma("sp", x[:, :n], XM3[:, k, n0:n0 + n], reads=[g.db["XM"]], writes=[x.b])
                t = tmp.next()
                pg.op("dve", lambda e, k=k, t=t, x=x: e.scalar_tensor_tensor(out=t[:, :n], in0=x[:, :n], scalar=cf6[:, 3, k, isctx:isctx + 1], in1=rs[:, :n],
                                                                              op0=ALU.mult, op1=ALU.mult), reads=[x.b, rs.b, cf6.b], writes=[t.b])
                pg.op("act", lambda e, k=k, t=t: e.activation(out=h2[:, k, :n], in_=t[:, :n], func=AF.Identity, bias=cf6[:, 4, k, isctx:isctx + 1], scale=1.0),
                      reads=[t.b, cf6.b], writes=[h2.b])
            for wbk in range(16):
                w = wtp.next()
                pg.dma("sp", w[:, :, :], WUP[:, :, wbk * 512:(wbk + 1) * 512], reads=[wu], writes=[w.b])
                for j in range(4):
                    m = wbk * 4 + j
                    ps = psr.next()
                    mm_group(pg, ps[:, :n], [(w[:, k, j * 128:(j + 1) * 128], h2[:, k, :n]) for k in range(KC)], reads=[w.b, h2.b], writes=[ps.b])
                    r = rl.next()
                    pg.op("act", lambda e, ps=ps, r=r: e.activation(out=r[:, :n], in_=ps[:, :n], func=AF.Relu), reads=[ps.b], writes=[r.b])
                    pg.op("dve", lambda e, r=r, m=m: e.tensor_tensor(out=h1[:, m, :n], in0=r[:, :n], in1=r[:, :n], op=ALU.mult), reads=[r.b], writes=[h1.b])
            ss2 = pss.next()
            for mg in range(4):
                acc = [psr.next() for _ in range(4)]
                for kg in range(4):
                    w = wtp.next()
                    pg.dma("sp", w[:, :, :], WDN[:, kg, :, mg * 512:(mg + 1) * 512], reads=[wd], writes=[w.b])
                    for j in range(4):
                        mm_group(pg, acc[j][:, :n], [(w[:, kk, j * 128:(j + 1) * 128], h1[:, kg * KC + kk, :n]) for kk in range(KC)],
                                 reads=[w.b, h1.b], writes=[acc[j].b], start=(kg == 0), stop=(kg == 3))
                for j in range(4):
                    m = mg * 4 + j
                    ps = acc[j]
                    pg.op("act", lambda e, ps=ps, m=m: e.activation(out=mo[:, m, :n], in_=ps[:, :n], func=AF.Copy), reads=[ps.b], writes=[mo.b])
                    sq = sqp.next()
                    pg.op("act", lambda e, ps=ps, sq=sq: e.activation(out=sq[:, :n], in_=ps[:, :n], func=AF.Square), reads=[ps.b], writes=[sq.b])
                    mm_group(pg, ss2[:, :n], [(g.ones[:, :], sq[:, :n])], reads=[sq.b, g.ones.b], writes=[ss2.b], start=(m == 0), stop=(m == 15))
            rs2 = rstd.next()
            t0 = tmp.next()
            rstd_from_ss(g, ss2, n, rs2, t0, float(D))
            for k in range(KC):
                x = xk.next()
                pg.dma("sp", x[:, :n], XM3[:, k, n0:n0 + n], reads=[g.db["XM"]], writes=[x.b])
                t = tmp.next()
                o = ok.next()
                pg.op("dve", lambda e, k=k, t=t: e.scalar_tensor_tensor(out=t[:, :n], in0=mo[:, k, :n], scalar=cf6[:, 5, k, isctx:isctx + 1], in1=rs2[:, :n],
                                                                         op0=ALU.mult, op1=ALU.mult), reads=[mo.b, rs2.b, cf6.b], writes=[t.b])
                pg.op("dve", lambda e, t=t, x=x, o=o: e.tensor_tensor(out=o[:, :n], in0=t[:, :n], in1=x[:, :n], op=ALU.add), reads=[t.b, x.b], writes=[o.b])
                if last:
                    pg.dma("pool", OUT3[:, k, n0 - CTX:n0 - CTX + n], o[:, :n], reads=[o.b], writes=[g.db["OUT"]])
                else:
                    pg.dma("pool", XO3[:, k, n0:n0 + n], o[:, :n], reads=[o.b], writes=[g.db[xout_name]])
        pg.barrier()


def rope_tables(seq):
    rows = seq // GRID_W
    row = np.repeat(np.arange(rows), GRID_W).astype(np.float32)
    col = np.tile(np.arange(GRID_W), rows).astype(np.float32)
    half = 32
    inv_freq = (10000.0 ** (-np.arange(half, dtype=np.float32) / half)).astype(np.float32)
    ang_r = row[:, None] * inv_freq[None, :]
    ang_c = col[:, None] * inv_freq[None, :]
    cos = np.zeros((128, seq), np.float32)
    sin = np.zeros((128, seq), np.float32)
    for base, ang in ((0, ang_r), (64, ang_c)):
        c = np.cos(ang).astype(np.float32).T
        s = np.sin(ang).astype(np.float32).T
        cos[base:base + 32] = c
        cos[base + 32:base + 64] = c
        sin[base:base + 32] = -s
        sin[base + 32:base + 64] = s
    return cos, sin


def const_mats():
    ident = np.eye(128, dtype=np.float32)
    jm = np.zeros((128, 128), np.float32)
    for n in range(64):
        jm[n, 64 + n] = 1.0
        jm[64 + n, n] = -1.0
    perm = np.zeros((128, 128), np.float32)
    for m in range(128):
        blk = (m // 32)
        partner = m + 32 if blk % 2 == 0 else m - 32
        perm[partner, m] = 1.0
    return ident, jm, perm


def prep_inputs(inp, seq, depth):
    L = depth
    f = lambda a: np.ascontiguousarray(np.asarray(a, dtype=np.float32))
    sh = {}
    for nm in ("w_mod", "w_in", "w_conv_out", "w_glu", "w_attn_out", "w_out", "w_up", "w_down"):
        sh[nm] = f(inp[nm])
    bm = f(inp["b_mod"]).reshape(L, 96, 128).transpose(0, 2, 1)
    sh["b_modT"] = np.ascontiguousarray(np.repeat(bm[:, :, :, None], 2, axis=3))
    for nm in ("g_pre_mix", "g_post_mix", "g_pre_mlp", "g_post_mlp"):
        sh[nm] = np.ascontiguousarray(f(inp[nm]).reshape(L, KC, 128).transpose(0, 2, 1))
    sh["conv_w"] = np.ascontiguousarray(f(inp["conv_w"]).reshape(L, 3, 8, 128).transpose(0, 3, 2, 1))
    lre = f(inp["ssm_lam_re"]).reshape(L, 96, NST)
    lim = f(inp["ssm_lam_im"]).reshape(L, 96, NST)
    ldt = f(inp["ssm_log_dt"]).reshape(L, 96)
    dup = lambda a: np.ascontiguousarray(np.concatenate([a.transpose(0, 2, 1), a.transpose(0, 2, 1)], axis=1))
    sh["lam_re_a"] = dup(lre)
    sh["lam_im_a"] = dup(lim)
    sh["log_dt_a"] = np.ascontiguousarray(np.broadcast_to(ldt[:, None, :], (L, 128, 96)))
    rep16 = lambda a: np.ascontiguousarray(np.broadcast_to(a.reshape(L, 1, 96 * NST), (L, 16, 96 * NST)))
    sh["lam_re_b"] = rep16(lre)
    sh["lam_im_b"] = rep16(lim)
    sh["log_dt_b"] = rep16(np.repeat(ldt[:, :, None], NST, axis=2))
    sh["b_re_b"] = np.ascontiguousarray(f(inp["ssm_b_re"]).reshape(L, 96, NST, 16).transpose(0, 3, 1, 2).reshape(L, 16, 96 * NST))
    sh["b_im_b"] = np.ascontiguousarray(f(inp["ssm_b_im"]).reshape(L, 96, NST, 16).transpose(0, 3, 1, 2).reshape(L, 16, 96 * NST))
    cre = f(inp["ssm_c_re"]).reshape(L, 96, 16, NST).transpose(0, 3, 1, 2).reshape(L, NST, 96 * 16)
    cim = f(inp["ssm_c_im"]).reshape(L, 96, 16, NST).transpose(0, 3, 1, 2).reshape(L, NST, 96 * 16)
    sh["c_ri"] = np.ascontiguousarray(np.concatenate([cre, cim], axis=1))
    sh["ssm_dT"] = np.ascontiguousarray(f(inp["ssm_d"]).reshape(L, NG, 16).transpose(0, 2, 1))
    sh["q_gain"] = np.ascontiguousarray(f(inp["q_gain"]).reshape(L, 128, 1))
    sh["k_gain"] = np.ascontiguousarray(f(inp["k_gain"]).reshape(L, 128, 1))
    cos, sin = rope_tables(seq)
    sh["rope_cos"] = cos
    sh["rope_sin"] = sin
    ident, jm, perm = const_mats()
    sh["c_ident"] = ident
    sh["c_jmat"] = jm
    sh["c_perm"] = perm
    x = f(inp["x"])
    ctx = f(inp["ctx"])
    c = f(inp["c"])
    cc = f(inp["c_ctx"])
    per = []
    for b in range(x.shape[0]):
        d = {}
        d["x0T"] = np.ascontiguousarray(np.concatenate([ctx[b].T, x[b].T], axis=1))
        ct = np.stack([c[b].reshape(KC, 128).T, cc.reshape(KC, 128).T], axis=2)
        d["cT"] = np.ascontiguousarray(ct)
        per.append(d)
    return sh, per


_CACHE = {}


def run_model(inp, seq, depth):
    key = (seq, depth)
    if key not in _CACHE:
        _CACHE[key] = build_program(seq, depth)
    nc, pg = _CACHE[key]
    sh, per = prep_inputs(inp, seq, depth)
    B = len(per)
    in_maps = []
    for core in range(8):
        m = dict(sh)
        m.update(per[core % B])
        in_maps.append(m)
    res = run_bass_kernel_spmd(nc, in_maps, core_ids=list(range(8)))
    outs = [np.ascontiguousarray(np.asarray(res.results[b]["outT"]).T) for b in range(B)]
    return np.stack(outs, axis=0).astype(np.float32)


def kernel(**inputs):
    return run_model(inputs, 8192, 4)
```
